# Optimizing a Trainium2 kernel written in Bass

```python
import math
import jax, jax.numpy as jnp
from jax import lax
import numpy as np

D_MODEL = 1024
BATCH = 32
SEQ = 2048
DEPTH = 4
DEC_BATCH = 8
DEC_SEQ = 4096
PAST_LEN = 128

F32 = jnp.float32
EPS = 1e-6
D_PLE = 256
D_FF = 2816
A_GROUPS = ((128, 1), (512, 4), (2048, 16))
A_HEADS_PER_GROUP = 4
A_HEAD_DIM = 64
A_HEADS = 12
A_WIDTH = 768
A_OUT = 256
A_BLOCK = 64
N_BUCKETS = 32
MAX_DISTANCE = 1024
B_HEADS = 4
B_HEAD_DIM = 128
B_WIDTH = 512
RET_CHUNK = 128
ROPE_BASE = 10000.0
C_HEADS = 4
C_KEY_DIM = 128
C_VAL_DIM = 128
C_WIDTH = 512
GLA_CHUNK = 64
SPLITS = (A_WIDTH, A_WIDTH, A_WIDTH, B_WIDTH, B_WIDTH, B_WIDTH, B_WIDTH, C_WIDTH, C_WIDTH, C_WIDTH, C_WIDTH, C_WIDTH)
N_IN = 3 * A_WIDTH + 4 * B_WIDTH + 5 * C_WIDTH

kernel_name = 'hybrid_bidir_dilated_retention_hgrn2_encoder'


def _rmsnorm(x, g):
    x32 = x.astype(F32)
    y = x32 * lax.rsqrt(jnp.mean(x32 * x32, axis=-1, keepdims=True) + EPS)
    return (y * g.astype(F32)).astype(x.dtype)


def _swiglu(u, wg, wu, wd):
    return (jax.nn.silu(u @ wg) * (u @ wu)) @ wd


def _t5_bucket(rel):
    half = N_BUCKETS // 2
    max_exact = half // 2
    ret = jnp.where(rel > 0, half, 0)
    n = jnp.abs(rel)
    nf = jnp.maximum(n, 1).astype(F32)
    large = max_exact + (jnp.log(nf / max_exact) / math.log(MAX_DISTANCE / max_exact) * (half - max_exact)).astype(jnp.int32)
    large = jnp.minimum(large, half - 1)
    return ret + jnp.where(n < max_exact, n, large)


def _dilated_band_attention(q, k, v, bias_table, dilation, n_side):
    Bn, S, H, hd = q.shape
    L = S // dilation
    C = A_BLOCK
    nC = -(-L // C)
    Lp = nC * C

    def to_sub(t):
        t = t.reshape(Bn, L, dilation, H, hd).transpose(0, 2, 1, 3, 4)
        t = jnp.pad(t, ((0, 0), (0, 0), (0, Lp - L), (0, 0), (0, 0)))
        return t.reshape(Bn, dilation, nC, C, H, hd)

    def band(t):
        tp = jnp.pad(t, ((0, 0), (0, 0), (1, 1), (0, 0), (0, 0), (0, 0)))
        return jnp.concatenate([tp[:, :, :-2], tp[:, :, 1:-1], tp[:, :, 2:]], axis=3)

    qs = to_sub(q)
    kb, vb = band(to_sub(k)), band(to_sub(v))
    qi = np.arange(C)[:, None]
    kj = np.arange(3 * C)[None, :]
    off = kj - C - qi
    bias = bias_table[_t5_bucket(jnp.asarray(off * dilation, dtype=jnp.int32))]
    bias = jnp.transpose(bias, (2, 0, 1)).astype(F32)
    kpos = (np.arange(nC)[:, None, None] - 1) * C + kj[None]
    valid = (np.abs(off) <= n_side)[None] & (kpos >= 0) & (kpos < L)
    s = jnp.einsum('brnqhd,brnkhd->brnhqk', qs, kb).astype(F32) * (hd ** -0.5) + bias
    s = jnp.where(valid[:, None], s, -1e30)
    m = jnp.max(s, axis=-1, keepdims=True)
    e = jnp.exp(s - m)
    den = jnp.sum(e, axis=-1, keepdims=True)
    o = jnp.einsum('brnhqk,brnkhd->brnqhd', (e / den).astype(v.dtype), vb)
    lse = jnp.swapaxes((m + jnp.log(den))[..., 0], 3, 4)

    def from_sub(t):
        t = t.reshape((Bn, dilation, Lp) + t.shape[4:])[:, :, :L]
        t = jnp.swapaxes(t, 1, 2)
        return t.reshape((Bn, S) + t.shape[3:])

    return from_sub(o), from_sub(lse)


def _mixer_a(q, k, v, rel_bias):
    Bn, S, _ = q.shape
    shp = (Bn, S, A_HEADS, A_HEAD_DIM)
    q, k, v = q.reshape(shp), k.reshape(shp), v.reshape(shp)
    outs, lses = [], []
    for g, (window, dil) in enumerate(A_GROUPS):
        sl = slice(g * A_HEADS_PER_GROUP, (g + 1) * A_HEADS_PER_GROUP)
        o, lse = _dilated_band_attention(q[:, :, sl], k[:, :, sl], v[:, :, sl], rel_bias[:, sl], dil, (window // 2) // dil)
        outs.append(o)
        lses.append(lse)
    w = jax.nn.softmax(jnp.stack(lses), axis=0)[..., None]
    out = jnp.sum(w * jnp.stack(outs).astype(F32), axis=0)
    return out.reshape(Bn, S, A_OUT).astype(q.dtype)


def _rotary(x, pos):
    half = x.shape[-1] // 2
    inv = ROPE_BASE ** (-jnp.arange(half, dtype=F32) / half)
    ang = pos[:, None] * inv[None]
    cos, sin = jnp.cos(ang)[None, :, None, :], jnp.sin(ang)[None, :, None, :]
    x1, x2 = x[..., :half], x[..., half:]
    return jnp.concatenate([x1 * cos - x2 * sin, x1 * sin + x2 * cos], axis=-1).astype(x.dtype)


def _retention_dir(q, k, v, log_gamma):
    Bn, S, H, dk = q.shape
    dv = v.shape[-1]
    C = RET_CHUNK
    n = S // C
    q = q.reshape(Bn, n, C, H, dk)
    k = k.reshape(Bn, n, C, H, dk)
    v = v.reshape(Bn, n, C, H, dv)
    idx = jnp.arange(C, dtype=F32)
    rel = idx[:, None] - idx[None, :]
    decay = jnp.where((rel >= 0)[..., None], jnp.exp(jnp.maximum(rel, 0.0)[..., None] * log_gamma), 0.0)
    att = jnp.einsum('bnihd,bnjhd->bnhij', q, k) * jnp.transpose(decay, (2, 0, 1))
    o = jnp.einsum('bnhij,bnjhe->bnihe', att, v)
    k_dec = k * jnp.exp((C - 1 - idx)[:, None] * log_gamma)[:, :, None]
    kv = jnp.einsum('bnjhd,bnjhe->nbhde', k_dec, v)
    g_chunk = jnp.exp(C * log_gamma)[:, None, None]

    def step(state, kv_c):
        return state * g_chunk + kv_c, state

    _, prev = lax.scan(step, jnp.zeros(kv.shape[1:], kv.dtype), kv)
    q_dec = q * jnp.exp((idx + 1)[:, None] * log_gamma)[:, :, None]
    o = o + jnp.einsum('bnihd,nbhde->bnihe', q_dec, prev)
    return o.reshape(Bn, S, H, dv)


def _mixer_b(q, k, v, g, gain):
    Bn, S, _ = q.shape
    shp = (Bn, S, B_HEADS, B_HEAD_DIM)
    q, k, v = q.reshape(shp), k.reshape(shp), v.reshape(shp)
    pos = jnp.arange(S, dtype=F32)
    q = _rotary(q, pos)
    k = _rotary(k, pos) * (B_HEAD_DIM ** -0.5)
    hh = jnp.arange(B_HEADS, dtype=F32)
    lg_fwd = jnp.log1p(-jnp.exp2(-5.0 - hh))
    lg_bwd = jnp.log1p(-jnp.exp2(-5.5 - hh))
    o = _retention_dir(q, k, v, lg_fwd) + _retention_dir(q[:, ::-1], k[:, ::-1], v[:, ::-1], lg_bwd)[:, ::-1]
    o = o.astype(F32)
    mu = jnp.mean(o, axis=-1, keepdims=True)
    var = jnp.mean(jnp.square(o - mu), axis=-1, keepdims=True)
    o = ((o - mu) * lax.rsqrt(var + EPS)).reshape(Bn, S, B_WIDTH) * gain.astype(F32)
    return (jax.nn.silu(g.astype(F32)) * o).astype(g.dtype)


def _gla_dir(q, k, v, logf):
    Bn, S, H, dk = q.shape
    dv = v.shape[-1]
    C = GLA_CHUNK
    n = S // C
    q = q.reshape(Bn, n, C, H, dk)
    k = k.reshape(Bn, n, C, H, dk)
    v = v.reshape(Bn, n, C, H, dv)
    b = jnp.cumsum(logf.reshape(Bn, n, C, H, dk), axis=2)
    b_ref = b[:, :, C // 2 - 1:C // 2]
    att = jnp.einsum('bnihd,bnjhd->bnhij', q * jnp.exp(b - b_ref), k * jnp.exp(b_ref - b))
    tri = jnp.tril(jnp.ones((C, C), dtype=bool))
    att = jnp.where(tri, att, 0.0)
    o = jnp.einsum('bnhij,bnjhe->bnihe', att, v)
    b_last = b[:, :, -1:]
    kv = jnp.einsum('bnjhd,bnjhe->nbhde', k * jnp.exp(b_last - b), v)
    g_chunk = jnp.moveaxis(jnp.exp(b_last[:, :, 0]), 1, 0)

    def step(state, inp):
        kv_c, g_c = inp
        return state * g_c[..., None] + kv_c, state

    _, prev = lax.scan(step, jnp.zeros(kv.shape[1:], kv.dtype), (kv, g_chunk))
    o = o + jnp.einsum('bnihd,nbhde->bnihe', q * jnp.exp(b), prev)
    return o.reshape(Bn, S, H, dv)


def _mixer_c(q, zf, zb, v, g, lb, gain):
    Bn, S, _ = q.shape
    kshp = (Bn, S, C_HEADS, C_KEY_DIM)
    q = jax.nn.silu(q).reshape(kshp)
    v = v.reshape(Bn, S, C_HEADS, C_VAL_DIM)
    log_lb, log_1mlb = jnp.log(lb), jnp.log1p(-lb)

    def gate(z):
        z = z.astype(F32)
        logf = jnp.logaddexp(log_lb, log_1mlb + jax.nn.log_sigmoid(z))
        key = jnp.exp(log_1mlb + jax.nn.log_sigmoid(-z))
        return logf.reshape(kshp), key.reshape(kshp)

    logf_f, k_f = gate(zf)
    logf_b, k_b = gate(zb)
    o = _gla_dir(q, k_f, v, logf_f) + _gla_dir(q[:, ::-1], k_b[:, ::-1], v[:, ::-1], logf_b[:, ::-1])[:, ::-1]
    o = o.astype(F32)
    o = o * lax.rsqrt(jnp.mean(o * o, axis=-1, keepdims=True) + EPS)
    o = o.reshape(Bn, S, C_WIDTH) * gain.astype(F32)
    return (jax.nn.silu(g.astype(F32)) * o).astype(g.dtype)


def _trunk(x, p, lbs, w):
    h = x
    cuts = [int(c) for c in np.cumsum(SPLITS)[:-1]]
    for l in range(DEPTH):
        h = h + 0.5 * _swiglu(_rmsnorm(h, w['ffn1_norm'][l]), w['ffn1_w_gate'][l], w['ffn1_w_up'][l], w['ffn1_w_down'][l])
        u = _rmsnorm(h, w['mix_norm'][l])
        z = u @ w['w_in'][l]
        qa, ka, va, qb, kb, vb, gb, qc, fcf, fcb, ic, gc = jnp.split(z, cuts, axis=-1)
        a = _mixer_a(qa, ka, va, w['rel_bias'])
        b = _mixer_b(qb, kb, vb, gb, w['ret_norm'][l])
        c = _mixer_c(qc, fcf, fcb, ic, gc, lbs[l], w['hgrn_norm'][l])
        ga, gbr, gcr = jnp.split(jax.nn.sigmoid(u @ w['w_merge_gate'][l]), 3, axis=-1)
        m = ga * (a @ w['w_branch_a'][l]) + gbr * (b @ w['w_branch_b'][l]) + gcr * (c @ w['w_branch_c'][l])
        h = h + m @ w['w_out'][l]
        h = h + 0.5 * _swiglu(_rmsnorm(h, w['ffn2_norm'][l]), w['ffn2_w_gate'][l], w['ffn2_w_up'][l], w['ffn2_w_down'][l])
        h = h + jax.nn.sigmoid(_rmsnorm(h, w['ple_norm'][l]) @ w['w_ple_gate'][l]) * (p[l] @ w['w_ple_proj'][l])
    return _rmsnorm(h, w['final_norm'])


def setup_inputs(seed: int = 0) -> dict:
    key = jax.random.key(seed)
    ks = iter(jax.random.split(key, 32))
    D, F = D_MODEL, D_FF

    def nrm(shape, scale):
        return jax.random.normal(next(ks), shape, F32) * scale

    def gain(shape):
        return 1.0 + 0.05 * jax.random.normal(next(ks), shape, F32)

    return {
        'x_prompt': nrm((BATCH, SEQ, D), 1.0),
        'x_sample': nrm((DEC_BATCH, DEC_SEQ, D), 1.0),
        'p_prompt': nrm((DEPTH, BATCH, SEQ, D_PLE), 1.0),
        'p_sample': nrm((DEPTH, DEC_BATCH, DEC_SEQ, D_PLE), 1.0),
        'ffn1_norm': gain((DEPTH, D)),
        'ffn1_w_gate': nrm((DEPTH, D, F), D ** -0.5),
        'ffn1_w_up': nrm((DEPTH, D, F), D ** -0.5),
        'ffn1_w_down': nrm((DEPTH, F, D), F ** -0.5),
        'mix_norm': gain((DEPTH, D)),
        'w_in': nrm((DEPTH, D, N_IN), D ** -0.5),
        'rel_bias': nrm((N_BUCKETS, A_HEADS), 0.5),
        'ret_norm': gain((DEPTH, B_WIDTH)),
        'hgrn_lower_bound': nrm((DEPTH, C_WIDTH), 1.0),
        'hgrn_norm': gain((DEPTH, C_WIDTH)),
        'w_branch_a': nrm((DEPTH, A_OUT, D), A_OUT ** -0.5),
        'w_branch_b': nrm((DEPTH, B_WIDTH, D), B_WIDTH ** -0.5),
        'w_branch_c': nrm((DEPTH, C_WIDTH, D), C_WIDTH ** -0.5),
        'w_merge_gate': nrm((DEPTH, D, 3 * D), D ** -0.5),
        'w_out': nrm((DEPTH, D, D), D ** -0.5),
        'ffn2_norm': gain((DEPTH, D)),
        'ffn2_w_gate': nrm((DEPTH, D, F), D ** -0.5),
        'ffn2_w_up': nrm((DEPTH, D, F), D ** -0.5),
        'ffn2_w_down': nrm((DEPTH, F, D), F ** -0.5),
        'ple_norm': gain((DEPTH, D)),
        'w_ple_gate': nrm((DEPTH, D, D), D ** -0.5),
        'w_ple_proj': nrm((DEPTH, D_PLE, D), D_PLE ** -0.5),
        'final_norm': gain((D,)),
    }


def reference(x_prompt, x_sample, p_prompt, p_sample, ffn1_norm, ffn1_w_gate, ffn1_w_up, ffn1_w_down, mix_norm, w_in, rel_bias, ret_norm, hgrn_lower_bound, hgrn_norm, w_branch_a, w_branch_b, w_branch_c, w_merge_gate, w_out, ffn2_norm, ffn2_w_gate, ffn2_w_up, ffn2_w_down, ple_norm, w_ple_gate, w_ple_proj, final_norm):
    lb = jax.nn.softmax(hgrn_lower_bound.astype(F32), axis=0)
    lb = jnp.cumsum(lb, axis=0)
    lbs = lb - lb[0]
    w = dict(ffn1_norm=ffn1_norm, ffn1_w_gate=ffn1_w_gate, ffn1_w_up=ffn1_w_up, ffn1_w_down=ffn1_w_down,
             mix_norm=mix_norm, w_in=w_in, rel_bias=rel_bias, ret_norm=ret_norm, hgrn_norm=hgrn_norm,
             w_branch_a=w_branch_a, w_branch_b=w_branch_b, w_branch_c=w_branch_c, w_merge_gate=w_merge_gate,
             w_out=w_out, ffn2_norm=ffn2_norm, ffn2_w_gate=ffn2_w_gate, ffn2_w_up=ffn2_w_up,
             ffn2_w_down=ffn2_w_down, ple_norm=ple_norm, w_ple_gate=w_ple_gate, w_ple_proj=w_ple_proj,
             final_norm=final_norm)
    y_prompt = _trunk(x_prompt, p_prompt, lbs, w)
    y_sample = _trunk(x_sample, p_sample, lbs, w)
    return (y_prompt, y_sample)
```

```python
import contextlib
import math
import numpy as np
import concourse.bass as bass
import concourse.mybir as mybir
from concourse.bass_utils import run_bass_kernel_spmd

F32 = mybir.dt.float32
BF16 = mybir.dt.bfloat16
AF = mybir.ActivationFunctionType
ALU = mybir.AluOpType
AX = mybir.AxisListType

D = 1024
KD = 8
FF = 2816
KF = 22
DPLE = 256
NIN = 6912
EPS = 1e-6
TT = 512


class Tile:
    __slots__ = ("name", "writer", "readers")

    def __init__(self, name=""):
        self.name = name
        self.writer = None
        self.readers = {}


class Buf:
    __slots__ = ("ap", "t")

    def __init__(self, ap, t=None, name=""):
        self.ap = ap
        self.t = t if t is not None else Tile(name)


class Prog:
    ENGS = ("pe", "act", "dve", "pool", "sp")

    def __init__(self, nc, n_dma_sems=16):
        self.nc = nc
        self.ops = {e: [] for e in self.ENGS}
        self.count = {e: 0 for e in self.ENGS}
        self.seen = {e: {} for e in self.ENGS}
        self.n_dma_sems = n_dma_sems
        self.dma_n = {q: 0 for q in ("sp", "pool", "act")}
        self.n_ops = 0

    def _need(self, eng, ev, waits):
        if ev is None:
            return
        key, val = ev
        if self.seen[eng].get(key, 0) >= val:
            return
        self.seen[eng][key] = val
        waits.append((key, val))

    def _deps(self, eng, reads, writes, same_key):
        waits = []
        for t in reads:
            if t.writer is not None:
                self._need(eng, t.writer, waits)
        for t in writes:
            if t.writer is not None and t.writer[0] != same_key:
                self._need(eng, t.writer, waits)
            for k, v in t.readers.items():
                if k != same_key:
                    self._need(eng, (k, v), waits)
        return waits

    def _commit(self, ev, reads, writes):
        k, v = ev
        for t in reads:
            if t.readers.get(k, 0) < v:
                t.readers[k] = v
        for t in writes:
            t.writer = ev
            t.readers = {}

    def op(self, eng, fn, reads=(), writes=(), inc=True):
        reads = [b.t if isinstance(b, Buf) else b for b in reads]
        writes = [b.t if isinstance(b, Buf) else b for b in writes]
        waits = self._deps(eng, reads, writes, same_key=(eng if eng == "pe" else None))
        if inc:
            self.count[eng] += 1
            ev = (eng, self.count[eng])
        else:
            ev = (eng, self.count[eng] + 1)
        self.ops[eng].append((waits, fn, (eng, 1) if inc else None))
        self._commit(ev, reads, writes)
        self.n_ops += 1

    def I(self, eng, method, reads=(), writes=(), inc=True, **kw):
        self.op(eng, lambda e: getattr(e, method)(**kw), reads=reads, writes=writes, inc=inc)

    def D(self, q, out, in_, reads=(), writes=()):
        self.dma(q, lambda e: e.dma_start(out=out, in_=in_), reads=reads, writes=writes)

    def dma(self, q, fn, reads=(), writes=()):
        reads = [b.t if isinstance(b, Buf) else b for b in reads]
        writes = [b.t if isinstance(b, Buf) else b for b in writes]
        n = self.dma_n[q]
        self.dma_n[q] = n + 1
        k = n % self.n_dma_sems
        key = f"dma_{q}_{k}"
        val = 16 * (n // self.n_dma_sems + 1)
        waits = self._deps(q, reads, writes, same_key=None)
        if val > 16:
            self._need(q, (key, val - 16), waits)
        self.ops[q].append((waits, fn, (key, 16)))
        self._commit((key, val), reads, writes)
        self.n_ops += 1

    def barrier(self):
        evs = []
        for q, n in self.dma_n.items():
            for k in range(min(n, self.n_dma_sems)):
                last_n = ((n - 1 - k) // self.n_dma_sems) * self.n_dma_sems + k
                evs.append((f"dma_{q}_{k}", 16 * (last_n // self.n_dma_sems + 1)))
        for e in self.ENGS:
            if self.count[e] > 0:
                evs.append((e, self.count[e]))
        for e in self.ENGS:
            waits = []
            for ev in evs:
                if ev[0] != e:
                    self._need(e, ev, waits)
            if waits:
                self.ops[e].append((waits, None, None))

    def emit(self):
        nc = self.nc
        keys = set()
        for e in self.ENGS:
            for waits, fn, inc in self.ops[e]:
                for k, v in waits:
                    keys.add(k)
                if inc is not None:
                    keys.add(inc[0])
        keys = sorted(keys)
        with contextlib.ExitStack() as st:
            sems = {k: st.enter_context(nc.semaphore(k)) for k in keys}
            block = st.enter_context(nc.Block())
            handles = {"pe": block.tensor, "act": block.scalar, "dve": block.vector,
                       "pool": block.gpsimd, "sp": block.sync}
            for e in self.ENGS:
                ops = self.ops[e]
                if not ops:
                    continue

                def body(h, ops=ops):
                    for waits, fn, inc in ops:
                        for k, v in waits:
                            h.wait_ge(sems[k], v)
                        if fn is not None:
                            ins = fn(h)
                            if inc is not None:
                                ins.then_inc(sems[inc[0]], inc[1])
                handles[e](body)


class Arena:
    def __init__(self, ap):
        self.base = ap
        self.W = ap.shape[1]
        self.off = 0

    def alloc(self, free_shape, dt, name="", parts=128):
        n = int(np.prod(free_shape))
        words = n if dt == F32 else (n + 1) // 2
        assert self.off + words <= self.W, f"arena overflow {name}: {self.off}+{words}>{self.W}"
        v = self.base[:, self.off:self.off + words]
        if dt != F32:
            v = v.bitcast(dt)[:, 0:n]
        self.off += words
        if len(free_shape) == 2:
            v = v.rearrange("p (a b) -> p a b", a=free_shape[0])
        elif len(free_shape) == 3:
            v = v.rearrange("p (a b c) -> p a b c", a=free_shape[0], b=free_shape[1])
        if parts != 128:
            v = v[0:parts]
        return Buf(v, name=name)

    def pool(self, n, free_shape, dt, name=""):
        return Rot([self.alloc(free_shape, dt, f"{name}{i}") for i in range(n)])


class Rot:
    def __init__(self, bufs):
        self.bufs = bufs
        self.i = 0

    def next(self):
        b = self.bufs[self.i % len(self.bufs)]
        self.i += 1
        return b


def _t5_bucket_np(rel):
    half, max_exact = 16, 8
    ret = np.where(rel > 0, half, 0)
    n = np.abs(rel)
    nf = np.maximum(n, 1).astype(np.float32)
    large = max_exact + (np.log(nf / max_exact) / math.log(1024 / max_exact) * (half - max_exact)).astype(np.int32)
    large = np.minimum(large, half - 1)
    return ret + np.where(n < max_exact, n, large)


def host_consts():
    c = {"c_ident": np.eye(128, dtype=np.float32)}
    oh = np.zeros((32, 3, 512), np.float32)
    band = np.zeros((12, 3, 512), np.float32)
    for g, d in enumerate((1, 4, 16)):
        for X in range(2):
            j = np.arange(256)
            off = (63 - j) if X == 0 else (191 - j)
            valid = (np.abs(off) <= 64) & (j <= 254)
            bk = _t5_bucket_np((off * d).astype(np.int32))
            for jj in range(256):
                if valid[jj]:
                    oh[bk[jj], g, X * 256 + jj] = 1.0
                    band[:, g, X * 256 + jj] = 1.0
    half = 64
    inv = (np.float32(10000.0) ** (-np.arange(half, dtype=np.float32) / np.float32(half))).astype(np.float32)
    pos = np.arange(4096, dtype=np.float32)
    ang = (pos[:, None] * inv[None]).astype(np.float32)
    cos, sin = np.cos(ang).astype(np.float32).T, np.sin(ang).astype(np.float32).T
    c["c_cos"] = np.concatenate([cos, cos], 0)
    c["c_sin"] = np.concatenate([-sin, sin], 0)
    sc = 128.0 ** -0.5
    retD = np.zeros((128, 4, 4, 128), np.float64)
    retdec = np.zeros((128, 8), np.float64)
    retq = np.zeros((128, 4, 2, 512), np.float64)
    i = np.arange(128)
    for h in range(4):
        gf = -np.expm1(np.log(2.0) * (-5.0 - h)); gb = -np.expm1(np.log(2.0) * (-5.5 - h))
        m, n = i[:, None], i[None, :]
        Dm = np.where(n > m, gf ** np.maximum(n - m, 0), np.where(n == m, 2.0, gb ** np.maximum(m - n, 0))) * sc
        retD[:, h, :, :] = Dm[:, None, :]
        retdec[:, 2 * h] = gf ** (127 - i)
        retdec[:, 2 * h + 1] = gb ** i
        retq[:, h, 0, :] = np.tile(gf ** (i + 1), 4)[None]
        retq[:, h, 1, :] = np.tile(gb ** (128 - i), 4)[None]
    c["c_retD"] = retD.reshape(128, 2048).astype(np.float32)
    c["c_retdec"] = retdec.astype(np.float32)
    c["c_retq"] = retq.reshape(128, 4096).astype(np.float32)
    rm = np.ones((128, 512), np.float32); rm[:, ::64] = 0.0
    c["c_reset"] = rm
    jj, ii = np.arange(64)[:, None], np.arange(64)[None, :]
    mf, mb = (ii >= jj).astype(np.float32), (ii <= jj).astype(np.float32)
    tri = np.zeros((128, 8, 64), np.float32)
    for cidx in range(4):
        tri[0:64, 2 * cidx] = mf
        tri[0:64, 2 * cidx + 1] = mb
    c["c_tri"] = tri.reshape(128, 512)
    c["c_onehot"] = oh.reshape(32, 1536)
    c["c_band"] = band.reshape(12, 1536)
    return c


class Builder:
    def __init__(self, seqs, depth, debug_outs=(), mode="ALL", depth_full=None):
        self.seqs = seqs
        self.depth = depth
        self.mode = mode
        self.depth_full = depth_full or depth
        self.NT = sum(s[0] for s in seqs)
        self.debug_outs = debug_outs
        self.nc = bass.Bass("TRN2", target_bir_lowering=False)
        self.P = Prog(self.nc)

    def mm(self, ps, pairs, extra_reads=()):
        n = len(pairs)
        for i, (l, r, rd) in enumerate(pairs):
            self.P.op("pe", lambda e, l=l, r=r, i=i: e.matmul(ps.ap, lhsT=l, rhs=r, start=(i == 0), stop=(i == n - 1)),
                      reads=list(rd) + list(extra_reads), writes=[ps], inc=(i == n - 1))

    def mm_ap(self, ps_ap, ps_buf, pairs):
        n = len(pairs)
        for i, (l, r, rd) in enumerate(pairs):
            self.P.op("pe", lambda e, l=l, r=r, i=i: e.matmul(ps_ap, lhsT=l, rhs=r, start=(i == 0), stop=(i == n - 1)),
                      reads=list(rd), writes=[ps_buf], inc=(i == n - 1))

    def ld_fm(self, dram, g0, T, buf, nk=KD, k0=0, q="sp"):
        for k in range(nk):
            self.P.dma(q, lambda e, k=k: e.dma_start(out=buf.ap[:, k, :], in_=dram[k0 + k, :, g0:g0 + T]), writes=[buf])

    def st_fm(self, dram, g0, T, buf, nk=KD, k0=0, q="sp"):
        for k in range(nk):
            self.P.dma(q, lambda e, k=k: e.dma_start(out=dram[k0 + k, :, g0:g0 + T], in_=buf.ap[:, k, :]), reads=[buf])

    def load_w(self, dst, src_ap, q="pool"):
        self.P.dma(q, lambda e: e.dma_start(out=dst.ap, in_=src_ap), writes=[dst])

    def rmsnorm(self, hT, gcol, uT, sqp, ps_ssq, rstd, ones_bf):
        P = self.P
        T = hT.ap.shape[2]
        pss = Buf(ps_ssq.ap[:, 0:T], t=ps_ssq.t)
        sqs = []
        for k in range(KD):
            sq = sqp.next()
            sqs.append(sq)
            P.op("act", lambda e, k=k, sq=sq: e.activation(out=sq.ap, in_=hT.ap[:, k, :], func=AF.Square),
                 reads=[hT], writes=[sq])
            P.op("pe", lambda e, k=k, sq=sq: e.matmul(pss.ap, lhsT=ones_bf.ap, rhs=sq.ap, start=(k == 0), stop=(k == KD - 1)),
                 reads=[sq, ones_bf], writes=[pss], inc=True)
        P.op("act", lambda e: e.activation(out=rstd.ap, in_=pss.ap, func=AF.Sqrt, bias=EPS, scale=1.0 / D),
             reads=[pss], writes=[rstd])
        P.op("dve", lambda e: e.reciprocal(out=rstd.ap, in_=rstd.ap), reads=[rstd], writes=[rstd])
        for k in range(KD):
            P.op("dve", lambda e, k=k: e.scalar_tensor_tensor(out=uT.ap[:, k, :], in0=hT.ap[:, k, :],
                                                              scalar=gcol[:, k:k + 1], in1=rstd.ap,
                                                              op0=ALU.mult, op1=ALU.mult),
                 reads=[hT, rstd], writes=[uT])

    def declare(self):
        nc = self.nc
        dp = self.depth
        n_p = sum(1 for s in self.seqs if s[1] == "p")
        n_s = sum(1 for s in self.seqs if s[1] == "s")
        SP = max([s[0] for s in self.seqs if s[1] == "p"] + [0])
        SS = max([s[0] for s in self.seqs if s[1] == "s"] + [0])
        self.io = {}

        def din(name, shape, dt=F32):
            self.io[name] = nc.dram_tensor(name, list(shape), dt, kind="ExternalInput").ap()

        def dout(name, shape, dt=F32):
            self.io[name] = nc.dram_tensor(name, list(shape), dt, kind="ExternalOutput").ap()

        mode = self.mode
        if n_p:
            if mode in ("ALL", "X"):
                din("x_prompt", [n_p, SP, D])
            if mode in ("ALL", "L"):
                din("p_prompt", [dp, n_p, SP, DPLE])
            if mode in ("ALL", "F"):
                dout("y_prompt", [n_p, SP, D])
        if n_s:
            if mode in ("ALL", "X"):
                din("x_sample", [n_s, SS, D])
            if mode in ("ALL", "L"):
                din("p_sample", [dp, n_s, SS, DPLE])
            if mode in ("ALL", "F"):
                dout("y_sample", [n_s, SS, D])
        if mode in ("ALL", "L"):
            for nm in ("ffn1", "ffn2"):
                din(f"{nm}_norm", [dp, D]); din(f"{nm}_w_gate", [dp, D, FF]); din(f"{nm}_w_up", [dp, D, FF])
                din(f"{nm}_w_down", [dp, FF, D])
            din("mix_norm", [dp, D]); din("w_in", [dp, D, NIN]); din("rel_bias", [32, 12])
            din("ret_norm", [dp, 512]); din("hgrn_lower_bound", [self.depth_full, 512]); din("hgrn_norm", [dp, 512])
            din("w_branch_a", [dp, 256, D]); din("w_branch_b", [dp, 512, D]); din("w_branch_c", [dp, 512, D])
            din("w_merge_gate", [dp, D, 3 * D]); din("w_out", [dp, D, D])
            din("ple_norm", [dp, D]); din("w_ple_gate", [dp, D, D]); din("w_ple_proj", [dp, DPLE, D])
            din("c_lmask", [128, 4])
        if mode in ("ALL", "F"):
            din("final_norm", [D])
        if mode in ("L", "F"):
            din("HT_in", [KD, 128, self.NT])
        if mode in ("L", "X"):
            dout("HT_out", [KD, 128, self.NT])
        din("c_ident", [128, 128])
        if mode in ("ALL", "L"):
            din("c_onehot", [32, 1536]); din("c_band", [12, 1536])
            din("c_cos", [128, 4096]); din("c_sin", [128, 4096]); din("c_retD", [128, 2048]); din("c_retdec", [128, 8])
            din("c_retq", [128, 4096]); din("c_reset", [128, 512]); din("c_tri", [128, 512])
        self.WV = nc.dram_tensor("WV", [12, 1536], F32).ap()
        self.ABUF = nc.dram_tensor("ABUF", [24, 128, 256], F32).ap()
        kind = "ExternalOutput" if "HT" in self.debug_outs else "Internal"
        self.HT = nc.dram_tensor("HT", [KD, 128, self.NT], F32, kind=kind).ap()
        kind = "ExternalOutput" if "UT" in self.debug_outs else "Internal"
        self.UT = nc.dram_tensor("UT", [KD, 128, self.NT], BF16, kind=kind).ap()
        kind = "ExternalOutput" if "ABC" in self.debug_outs else "Internal"
        self.ABC = nc.dram_tensor("ABC", [10, 128, self.NT], BF16, kind=kind).ap()

    def seq_offsets(self):
        offs, o = [], 0
        for s in self.seqs:
            offs.append(o)
            o += s[0]
        return offs

    def tiles(self, TT=TT):
        out = []
        for si, (S, _, _) in enumerate(self.seqs):
            for t0 in range(0, S, TT):
                out.append((si, t0, self.offs[si] + t0))
        return out

    GROWS = [("ffn1_norm", 8), ("mix_norm", 8), ("ffn2_norm", 8), ("ple_norm", 8), ("final_norm", 8),
             ("ret_norm", 4), ("hgrn_norm", 4), ("hgrn_lower_bound", 4)]

    def gcol(self, name, l, k0=0, n=8):
        per = dict(self.GROWS)[name]
        base = self.gbase[name] + (l * per if name != "final_norm" else 0)
        return self.G.ap[:, base + k0: base + k0 + n]

    def load_consts(self, ar, psb):
        P, io = self.P, self.io
        dp = self.depth
        self.ident = ar.alloc([128], F32, "ident")
        self.identb = ar.alloc([128], BF16, "identb")
        self.ones_bf = ar.alloc([128], BF16, "ones")
        P.dma("sp", lambda e: e.dma_start(out=self.ident.ap, in_=io["c_ident"]), writes=[self.ident])
        P.op("dve", lambda e: e.tensor_copy(out=self.identb.ap, in_=self.ident.ap), reads=[self.ident], writes=[self.identb])
        P.op("dve", lambda e: e.memset(self.ones_bf.ap, 1.0), writes=[self.ones_bf])
        self.gbase, rows = {}, 0
        srcs = []
        for name, per in self.GROWS:
            if name not in io:
                continue
            self.gbase[name] = rows
            n = per * (1 if name == "final_norm" else self.depth_full if name == "hgrn_lower_bound" else dp)
            ap = io[name]
            if name == "final_norm":
                src = ap.rearrange("(k p) -> k p", p=128)
            else:
                src = ap.rearrange("l (k p) -> (l k) p", p=128)
            srcs.append((rows, n, src))
            rows += n
        if rows == 0:
            return
        self.G = ar.alloc([rows], F32, "G")
        nblk = (rows + 127) // 128
        stage = ar.alloc([nblk, 128], F32, "gstage")
        P.op("dve", lambda e: e.memset(stage.ap, 0.0), writes=[stage])
        for (r0, n, src) in srcs:
            r = r0
            while r < r0 + n:
                blk, pr = divmod(r, 128)
                cnt = min(128 - pr, r0 + n - r)
                P.dma("sp", lambda e, blk=blk, pr=pr, cnt=cnt, src=src, a=r - r0:
                      e.dma_start(out=stage.ap[pr:pr + cnt, blk, :], in_=src[a:a + cnt, :]), writes=[stage])
                r += cnt
        for blk in range(nblk):
            ncol = min(128, rows - blk * 128)
            ps = psb[blk % len(psb)]
            P.op("pe", lambda e, blk=blk, ps=ps: e.transpose(ps.ap[:, 0:128], stage.ap[:, blk, :], self.ident.ap),
                 reads=[stage, self.ident], writes=[ps])
            P.op("dve", lambda e, blk=blk, ps=ps, ncol=ncol: e.tensor_copy(out=self.G.ap[:, blk * 128: blk * 128 + ncol],
                                                                            in_=ps.ap[:, 0:ncol]),
                 reads=[ps], writes=[self.G])

    def load_att_consts(self, ar, psb):
        P, io = self.P, self.io
        self.onesH = [ar.alloc([128], BF16, f"onesH{i}") for i in range(2)]
        for i in range(2):
            P.I("dve", "memset", ap=self.onesH[i].ap, constant=0.0, writes=[self.onesH[i]])
            P.I("dve", "memset", ap=self.onesH[i].ap[:, i * 64:(i + 1) * 64], constant=1.0, writes=[self.onesH[i]])
        self.Eatt = {(g, hp): ar.alloc([512], BF16, f"E{g}{hp}") for g in range(3) for hp in range(2)}
        mark = ar.off
        tb = ar.alloc([12], F32, "tb")
        oh = ar.alloc([1536], F32, "oh")
        band = ar.alloc([1536], F32, "band")
        wv = ar.alloc([1536], F32, "wv")
        P.D("sp", tb.ap[0:32, :], io["rel_bias"], writes=[tb])
        P.D("sp", oh.ap[0:32, :], io["c_onehot"], writes=[oh])
        P.D("sp", band.ap[0:12, :], io["c_band"], writes=[band])
        for g in range(3):
            ps = psb[g]
            P.I("pe", "matmul", out=ps.ap[0:12, :], lhsT=tb.ap[0:32, 0:12], rhs=oh.ap[0:32, g * 512:(g + 1) * 512],
                start=True, stop=True, reads=[tb, oh], writes=[ps])
            P.I("act", "activation", out=wv.ap[0:12, g * 512:(g + 1) * 512], in_=ps.ap[0:12, :], func=AF.Exp, reads=[ps], writes=[wv])
        P.I("dve", "tensor_tensor", out=wv.ap[0:12, :], in0=wv.ap[0:12, :], in1=band.ap[0:12, :], op=ALU.mult, reads=[wv, band], writes=[wv])
        wvd = Buf(self.WV, name="WVd")
        abd = Buf(self.ABUF, name="ABUFd")
        import os
        LV = int(os.environ.get("CLV", "9"))
        if LV >= 2:
            P.D("sp", self.WV, wv.ap[0:12, :], reads=[wv], writes=[wvd])
        for g in range(3):
            for i in range(4):
                for X in range(2):
                    if LV < 3:
                        continue
                    src = bass.AP(self.WV.tensor, (4 * g + i) * 1536 + g * 512 + X * 256, [[0, 128], [1, 256]])
                    P.D("sp", self.ABUF[(g * 4 + i) * 2 + X, :, :], src, reads=[wvd], writes=[Tile()])
        P.barrier()
        if LV < 4:
            ar.off = mark
            return
        est = ar.pool(2, [512], F32, "estage")
        for g in range(3):
            for hp in range(2):
                E = self.Eatt[(g, hp)]
                es = est.next()
                for ii in range(2):
                    for X in range(2):
                        src = bass.AP(self.ABUF.tensor, ((g * 4 + 2 * hp + ii) * 2 + X) * 128 * 256 + 127, [[255, 128], [1, 128]])
                        P.D("sp", es.ap[:, (2 * ii + X) * 128:(2 * ii + X + 1) * 128], src, writes=[es])
                P.I("dve", "tensor_copy", out=E.ap, in_=es.ap, reads=[es], writes=[E])
        P.barrier()
        ar.off = mark

    def phase_x(self, ar, psb):
        import os
        LV = int(os.environ.get("XLV", "9"))
        P, io = self.P, self.io
        mark = ar.off
        xin = ar.pool(3, [D], F32, "xin")
        hpool = ar.pool(2, [KD, TT], F32, "hT")
        pst = Rot(psb)
        for (si, t0, g0) in self.tiles(TT):
            S, kind, idx = self.seqs[si]
            xsrc = io["x_prompt" if kind == "p" else "x_sample"]
            hT = hpool.next()
            for sub in range(TT // 128):
                xb = xin.next()
                P.dma("sp", lambda e, xb=xb, sub=sub, xsrc=xsrc, idx=idx, t0=t0:
                      e.dma_start(out=xb.ap, in_=xsrc[idx, t0 + sub * 128: t0 + (sub + 1) * 128, :]), writes=[xb])
                for kk in range(0, KD, 4):
                    if LV < 2:
                        continue
                    ps = pst.next()
                    for j in range(4):
                        P.op("pe", lambda e, ps=ps, xb=xb, j=j, kk=kk: e.transpose(ps.ap[:, j * 128:(j + 1) * 128],
                                                                                   xb.ap[:, (kk + j) * 128:(kk + j + 1) * 128],
                                                                                   self.ident.ap),
                             reads=[xb, self.ident], writes=[ps])
                    if LV < 3:
                        continue
                    eng = "act" if (kk // 4) % 2 == 0 else "dve"
                    dst = hT.ap[:, kk:kk + 4, sub * 128:(sub + 1) * 128]
                    src = ps.ap.rearrange("p (j t) -> p j t", j=4)
                    if eng == "act":
                        P.op("act", lambda e, dst=dst, src=src: e.activation(out=dst, in_=src, func=AF.Copy), reads=[ps], writes=[hT])
                    else:
                        P.op("dve", lambda e, dst=dst, src=src: e.tensor_copy(out=dst, in_=src), reads=[ps], writes=[hT])
            self.st_fm(self.HT, g0, TT, hT)
        P.barrier()
        ar.off = mark

    def phase_ffn(self, l, which, ar, psb):
        P, io = self.P, self.io
        TF = 256
        nm = f"ffn{which}"
        mark = ar.off
        Wg = [ar.alloc([FF], BF16, f"wg{k}") for k in range(KD)]
        Wu = [ar.alloc([FF], BF16, f"wu{k}") for k in range(KD)]
        Wd = [ar.alloc([D], BF16, f"wd{k}") for k in range(KF)]
        for k in range(KD):
            self.load_w(Wg[k], io[f"{nm}_w_gate"][l, k * 128:(k + 1) * 128, :])
            self.load_w(Wu[k], io[f"{nm}_w_up"][l, k * 128:(k + 1) * 128, :])
        for k in range(KF):
            self.load_w(Wd[k], io[f"{nm}_w_down"][l, k * 128:(k + 1) * 128, :])
        hpool = ar.pool(2, [KD, TF], F32, "hT")
        upool = ar.pool(2, [KD, TF], BF16, "uT")
        sqp = ar.pool(2, [TF], BF16, "sq")
        rstd = ar.alloc([TF], F32, "rstd")
        aT = [ar.alloc([11, TF], BF16, f"aT{i}") for i in range(2)]
        sgp = ar.pool(2, [TF], F32, "sg")
        psgu, psd = Rot(psb[0:3]), Rot(psb[3:5])
        ps_ssq = psb[5]
        gam = self.gcol(f"{nm}_norm", l)
        gmix = self.gcol("mix_norm", l)
        import os
        LV = int(os.environ.get("ALV", "9"))
        for (si, t0, g0) in self.tiles(TF):
            if LV < 2:
                continue
            hT = hpool.next()
            uT = upool.next()
            self.ld_fm(self.HT, g0, TF, hT)
            self.rmsnorm(hT, gam, uT, sqp, ps_ssq, rstd, self.ones_bf)
            if LV < 3:
                self.st_fm(self.UT, g0, TF, uT)
                continue

            def gu(f):
                pb = psgu.next()
                self.mm_ap(pb.ap[:, 0:TF], pb, [(Wg[k].ap[:, f * 128:(f + 1) * 128], uT.ap[:, k, :], [Wg[k], uT]) for k in range(KD)])
                self.mm_ap(pb.ap[:, TF:2 * TF], pb, [(Wu[k].ap[:, f * 128:(f + 1) * 128], uT.ap[:, k, :], [Wu[k], uT]) for k in range(KD)])
                sg = sgp.next()
                P.op("act", lambda e: e.activation(out=sg.ap, in_=pb.ap[:, 0:TF], func=AF.Silu), reads=[pb], writes=[sg])
                a = aT[f // 11]
                P.op("dve", lambda e: e.tensor_tensor(out=a.ap[:, f % 11, :], in0=pb.ap[:, TF:2 * TF], in1=sg.ap, op=ALU.mult),
                     reads=[pb, sg], writes=[a])

            def down(half):
                a = aT[half]
                for m in range(0, KD, 2):
                    pd = psd.next()
                    for mm_ in range(2):
                        self.mm_ap(pd.ap[:, mm_ * TF:(mm_ + 1) * TF], pd,
                                   [(Wd[half * 11 + j].ap[:, (m + mm_) * 128:(m + mm_ + 1) * 128], a.ap[:, j, :], [Wd[half * 11 + j], a])
                                    for j in range(11)])
                    if LV == 5:
                        P.op("dve", lambda e, pd=pd, m=m, hT=hT: e.scalar_tensor_tensor(
                            out=hT.ap[:, m:m + 2, :], in0=pd.ap.rearrange("p (a t) -> p a t", a=2), scalar=0.5,
                            in1=hT.ap[:, m:m + 2, :], op0=ALU.mult, op1=ALU.add), reads=[pd, hT], writes=[hT])
                    else:
                        for mm_ in range(2):
                            P.I("dve", "scalar_tensor_tensor", out=hT.ap[:, m + mm_, :], in0=pd.ap[:, mm_ * TF:(mm_ + 1) * TF],
                                scalar=0.5, in1=hT.ap[:, m + mm_, :], op0=ALU.mult, op1=ALU.add, reads=[pd, hT], writes=[hT])
            for f in range(0, 13):
                gu(f)
            if LV >= 4:
                down(0)
            for f in range(13, KF):
                gu(f)
            if LV >= 4:
                down(1)
            if which == 1:
                u2 = upool.next()
                self.rmsnorm(hT, gmix, u2, sqp, ps_ssq, rstd, self.ones_bf)
                self.st_fm(self.UT, g0, TF, u2)
            self.st_fm(self.HT, g0, TF, hT)
        P.barrier()
        ar.off = mark

    def phase_merge(self, l, ar, psb):
        P, io = self.P, self.io
        mark = ar.off
        Wmg = [ar.alloc([3 * D], BF16, f"wmg{k}") for k in range(KD)]
        Wbr = [ar.alloc([D], BF16, f"wbr{k}") for k in range(10)]
        Wo = [ar.alloc([D], BF16, f"wo{k}") for k in range(KD)]
        for k in range(KD):
            self.load_w(Wmg[k], io["w_merge_gate"][l, k * 128:(k + 1) * 128, :])
        for k in range(10):
            nm, kk = (("w_branch_a", k) if k < 2 else ("w_branch_b", k - 2) if k < 6 else ("w_branch_c", k - 6))
            self.load_w(Wbr[k], io[nm][l, kk * 128:(kk + 1) * 128, :])
        for k in range(KD):
            self.load_w(Wo[k], io["w_out"][l, k * 128:(k + 1) * 128, :])
        hpool = ar.pool(2, [KD, TT], F32, "hT")
        upool = ar.pool(2, [KD, TT], BF16, "uT")
        bpool = ar.pool(2, [10, TT], BF16, "abcT")
        mT = ar.alloc([KD, TT], BF16, "mT")
        sgp = ar.pool(2, [TT], F32, "sg")
        tmpp = ar.pool(2, [TT], F32, "tmp")
        accp = ar.pool(2, [TT], F32, "acc")
        psg, psbr, pso = Rot(psb[0:2]), Rot(psb[2:4]), Rot(psb[4:6])
        brk = [(0, 2), (2, 6), (6, 10)]
        for (si, t0, g0) in self.tiles(TT):
            hT, uT, bT = hpool.next(), upool.next(), bpool.next()
            self.ld_fm(self.HT, g0, TT, hT)
            self.ld_fm(self.UT, g0, TT, uT)
            self.ld_fm(self.ABC, g0, TT, bT, nk=10)
            for m in range(KD):
                acc = accp.next()
                for x in range(3):
                    pg, pb = psg.next(), psbr.next()
                    c0 = x * D + m * 128
                    self.mm(pg, [(Wmg[k].ap[:, c0:c0 + 128], uT.ap[:, k, :], [Wmg[k], uT]) for k in range(KD)])
                    self.mm(pb, [(Wbr[k].ap[:, m * 128:(m + 1) * 128], bT.ap[:, k, :], [Wbr[k], bT]) for k in range(*brk[x])])
                    sg = sgp.next()
                    P.I("act", "activation", out=sg.ap, in_=pg.ap, func=AF.Sigmoid, reads=[pg], writes=[sg])
                    if x == 0:
                        P.I("dve", "tensor_tensor", out=acc.ap, in0=pb.ap, in1=sg.ap, op=ALU.mult, reads=[pb, sg], writes=[acc])
                    else:
                        tmp = tmpp.next()
                        P.I("dve", "tensor_tensor", out=tmp.ap, in0=pb.ap, in1=sg.ap, op=ALU.mult, reads=[pb, sg], writes=[tmp])
                        if x == 1:
                            P.I("dve", "tensor_tensor", out=acc.ap, in0=acc.ap, in1=tmp.ap, op=ALU.add, reads=[acc, tmp], writes=[acc])
                        else:
                            P.I("dve", "tensor_tensor", out=mT.ap[:, m, :], in0=acc.ap, in1=tmp.ap, op=ALU.add,
                                reads=[acc, tmp], writes=[mT])
            for m2 in range(KD):
                po = pso.next()
                self.mm(po, [(Wo[k].ap[:, m2 * 128:(m2 + 1) * 128], mT.ap[:, k, :], [Wo[k], mT]) for k in range(KD)])
                P.I("dve", "tensor_tensor", out=hT.ap[:, m2, :], in0=po.ap, in1=hT.ap[:, m2, :], op=ALU.add, reads=[po, hT], writes=[hT])
            self.st_fm(self.HT, g0, TT, hT)
        P.barrier()
        ar.off = mark

    def phase_ple(self, l, ar, psb, last):
        P, io = self.P, self.io
        mark = ar.off
        Wpg = [ar.alloc([D], BF16, f"wpg{k}") for k in range(KD)]
        Wpp = [ar.alloc([D], BF16, f"wpp{k}") for k in range(2)]
        for k in range(KD):
            self.load_w(Wpg[k], io["w_ple_gate"][l, k * 128:(k + 1) * 128, :])
        for k in range(2):
            self.load_w(Wpp[k], io["w_ple_proj"][l, k * 128:(k + 1) * 128, :])
        hpool = ar.pool(2, [KD, TT], F32, "hT")
        upool = ar.pool(2, [KD, TT], BF16, "uT")
        sqp = ar.pool(2, [TT], BF16, "sq")
        rstd = ar.alloc([TT], F32, "rstd")
        pin = ar.pool(4, [DPLE], F32, "pin")
        pT = ar.alloc([2, TT], BF16, "pT")
        sgp = ar.pool(2, [TT], F32, "sg")
        tmpp = ar.pool(2, [TT], F32, "tmp")
        if last:
            yT = ar.alloc([KD, TT], F32, "yT")
            ytok = ar.pool(2, [D], F32, "ytok")
        psg, psp, pst = Rot(psb[0:2]), Rot(psb[2:3]), Rot(psb[3:5])
        ps_ssq = psb[5]
        gam = self.gcol("ple_norm", l)
        for (si, t0, g0) in self.tiles(TT):
            S, kind, idx = self.seqs[si]
            psrc = io["p_prompt" if kind == "p" else "p_sample"]
            hT, uT = hpool.next(), upool.next()
            self.ld_fm(self.HT, g0, TT, hT)
            pbs = []
            for sub in range(TT // 128):
                pb = pin.next()
                pbs.append(pb)
                P.D("sp", pb.ap, psrc[l, idx, t0 + sub * 128:t0 + (sub + 1) * 128, :], writes=[pb])
            for c in range(2):
                ps = pst.next()
                for sub in range(TT // 128):
                    P.I("pe", "transpose", out=ps.ap[:, sub * 128:(sub + 1) * 128], in_=pbs[sub].ap[:, c * 128:(c + 1) * 128],
                        identity=self.ident.ap, reads=[pbs[sub], self.ident], writes=[ps])
                P.I("act", "activation", out=pT.ap[:, c, :], in_=ps.ap, func=AF.Copy, reads=[ps], writes=[pT])
            self.rmsnorm(hT, gam, uT, sqp, ps_ssq, rstd, self.ones_bf)
            for m in range(KD):
                pg, pp = psg.next(), psp.next()
                self.mm(pg, [(Wpg[k].ap[:, m * 128:(m + 1) * 128], uT.ap[:, k, :], [Wpg[k], uT]) for k in range(KD)])
                self.mm(pp, [(Wpp[k].ap[:, m * 128:(m + 1) * 128], pT.ap[:, k, :], [Wpp[k], pT]) for k in range(2)])
                sg, tmp = sgp.next(), tmpp.next()
                P.I("act", "activation", out=sg.ap, in_=pg.ap, func=AF.Sigmoid, reads=[pg], writes=[sg])
                P.I("dve", "tensor_tensor", out=tmp.ap, in0=pp.ap, in1=sg.ap, op=ALU.mult, reads=[pp, sg], writes=[tmp])
                P.I("dve", "tensor_tensor", out=hT.ap[:, m, :], in0=hT.ap[:, m, :], in1=tmp.ap, op=ALU.add, reads=[hT, tmp], writes=[hT])
            if not last:
                self.st_fm(self.HT, g0, TT, hT)
                continue
            self.final_tile(hT, yT, ytok, sqp, ps_ssq, rstd, pst, kind, idx, t0)
        P.barrier()
        ar.off = mark

    def final_tile(self, hT, yT, ytok, sqp, ps_ssq, rstd, pst, kind, idx, t0):
        P, io = self.P, self.io
        self.rmsnorm(hT, self.gcol("final_norm", 0), yT, sqp, ps_ssq, rstd, self.ones_bf)
        ydst = io["y_prompt" if kind == "p" else "y_sample"]
        for sub in range(TT // 128):
            yt = ytok.next()
            for kk in range(0, KD, 4):
                ps = pst.next()
                for j in range(4):
                    P.I("pe", "transpose", out=ps.ap[:, j * 128:(j + 1) * 128], in_=yT.ap[:, kk + j, sub * 128:(sub + 1) * 128],
                        identity=self.ident.ap, reads=[yT, self.ident], writes=[ps])
                if kk == 0:
                    P.I("act", "activation", out=yt.ap[:, 0:512], in_=ps.ap, func=AF.Copy, reads=[ps], writes=[yt])
                else:
                    P.I("dve", "tensor_copy", out=yt.ap[:, 512:1024], in_=ps.ap, reads=[ps], writes=[yt])
            P.D("sp", ydst[idx, t0 + sub * 128:t0 + (sub + 1) * 128, :], yt.ap, reads=[yt])

    def phase_final(self, ar, psb):
        P = self.P
        mark = ar.off
        hpool = ar.pool(2, [KD, TT], F32, "hT")
        sqp = ar.pool(2, [TT], BF16, "sq")
        rstd = ar.alloc([TT], F32, "rstd")
        yT = ar.alloc([KD, TT], F32, "yT")
        ytok = ar.pool(2, [D], F32, "ytok")
        pst = Rot(psb[0:4])
        ps_ssq = psb[5]
        for (si, t0, g0) in self.tiles(TT):
            S, kind, idx = self.seqs[si]
            hT = hpool.next()
            self.ld_fm(self.HT, g0, TT, hT)
            self.final_tile(hT, yT, ytok, sqp, ps_ssq, rstd, pst, kind, idx, t0)
        P.barrier()
        ar.off = mark

    def copy_ht(self, src, dst):
        P = self.P
        for k in range(KD):
            P.D("sp", dst[k, :, :], src[k, :, :], reads=[], writes=[Tile()])
        P.barrier()

    def phase_mix(self, l, ar, psb):
        import os
        which = os.environ.get("MIXERS", "abc")
        for si in range(len(self.seqs)):
            if "a" in which:
                self.mixer_a(l, si, ar, psb)
            if "b" in which:
                self.mixer_b(l, si, ar, psb)
            if "c" in which:
                self.mixer_c(l, si, ar, psb)

    def proj_tile(self, ps, W, col0, uT, ncol=128):
        T = uT.ap.shape[2]
        self.mm_ap(ps.ap[0:ncol, 0:T], ps, [(W[k].ap[:, col0:col0 + ncol], uT.ap[:, k, :], [W[k], uT]) for k in range(KD)])

    def mixer_a(self, l, si, ar, psb):
        P, io = self.P, self.io
        S, kind, idx = self.seqs[si]
        off = self.offs[si]
        mark = ar.off
        NB = S // 128
        WA = [ar.alloc([2304], BF16, f"wa{k}") for k in range(KD)]
        for k in range(KD):
            self.load_w(WA[k], io["w_in"][l, k * 128:(k + 1) * 128, 0:2304])
        upool = ar.pool(2, [KD, TT], BF16, "uT")
        qkv = [ar.alloc([S + 128], BF16, f"qkvT{x}") for x in range(4)]
        Vp = [ar.alloc([NB + 1, 128], BF16, f"Vp{i}") for i in range(2)]
        acc_n = ar.alloc([S], F32, "acc_n")
        acc_d = ar.alloc([S], F32, "acc_d")
        pexp_p = ar.pool(2, [512], BF16, "pexp")
        pT_p = ar.pool(2, [512], BF16, "pT")
        aout = ar.alloc([S], BF16, "aout")
        psp, pss = Rot(psb[0:2]), Rot(psb[2:4])
        ps_o, ps_d = psb[4], psb[5]
        pst = Rot(self.psbf)
        for x in range(4):
            P.I("pool", "memset", ap=qkv[x].ap, constant=0.0, writes=[qkv[x]])
        for i in range(2):
            P.I("pool", "memset", ap=Vp[i].ap, constant=0.0, writes=[Vp[i]])
        ev = 0
        import os
        LV = int(os.environ.get("MLV", "9"))
        for hp in range(2):
            for g, d in enumerate((1, 4, 16)):
                L = S // d
                for t0 in range(0, S, TT):
                    uT = upool.next()
                    self.ld_fm(self.UT, off + t0, TT, uT)
                    for x in range(3):
                        ps = psp.next()
                        self.proj_tile(ps, WA, x * 768 + 256 * g + 128 * hp, uT)
                        for (bufi, pr) in (((0, slice(0, 64)), (3, slice(64, 128))) if x == 0 else ((x, slice(0, 128)),)):
                            if d == 1:
                                src = ps.ap[pr, :]
                                dst = qkv[bufi].ap[pr, 64 + t0:64 + t0 + TT]
                            else:
                                src = ps.ap[pr, :].rearrange("p (l r) -> p r l", r=d)
                                dst = qkv[bufi].ap[pr, 64:64 + S].rearrange("p (r l) -> p r l", r=d)[:, :, t0 // d:(t0 + TT) // d]
                            if ev % 2 == 0:
                                P.I("act", "activation", out=dst, in_=src, func=AF.Copy, reads=[ps], writes=[qkv[bufi]])
                            else:
                                P.I("dve", "tensor_copy", out=dst, in_=src, reads=[ps], writes=[qkv[bufi]])
                            ev += 1
                for c0 in range(0, NB + 1, 4):
                    if LV < 2:
                        continue
                    nb = min(4, NB + 1 - c0)
                    ps = pst.next()
                    psv = ps.ap
                    for j in range(nb):
                        P.I("pe", "transpose", out=psv[:, j * 128:(j + 1) * 128], in_=qkv[2].ap[:, (c0 + j) * 128:(c0 + j + 1) * 128],
                            identity=self.identb.ap, reads=[qkv[2], self.identb], writes=[ps])
                    for i in range(2):
                        for j in range(nb):
                            src = psv[:, j * 128 + i * 64:j * 128 + (i + 1) * 64]
                            dst = Vp[i].ap[:, c0 + j, i * 64:(i + 1) * 64]
                            if (i + j) % 2 == 0:
                                P.I("act", "activation", out=dst, in_=src, func=AF.Copy, reads=[ps], writes=[Vp[i]])
                            else:
                                P.I("dve", "tensor_copy", out=dst, in_=src, reads=[ps], writes=[Vp[i]])
                E = self.Eatt[(g, hp)]
                nbl = L // 128
                grp = min(4, nbl)
                for r in range(d):
                    for jq in range(nbl):
                        if LV < 3:
                            continue
                        c = (r * L) // 128 + jq
                        qc = 64 + r * L + 128 * jq
                        first, lastb = (jq == 0), (jq == nbl - 1)
                        ps_s = pss.next()
                        for ii in range(2):
                            qb = qkv[0] if ii == 0 else qkv[3]
                            for X in range(2):
                                P.I("pe", "matmul", out=ps_s.ap[:, (2 * ii + X) * 128:(2 * ii + X + 1) * 128],
                                    lhsT=qkv[1].ap[:, 128 * (c + X):128 * (c + X + 1)], rhs=qb.ap[:, qc:qc + 128],
                                    start=True, stop=True, reads=[qb, qkv[1]], writes=[ps_s])
                        pexp, pT = pexp_p.next(), pT_p.next()
                        P.I("act", "activation", out=pexp.ap, in_=ps_s.ap, func=AF.Exp, scale=0.125, reads=[ps_s], writes=[pexp])
                        P.I("dve", "tensor_tensor", out=pT.ap, in0=pexp.ap, in1=E.ap, op=ALU.mult, reads=[pexp, E], writes=[pT])
                        if first:
                            for ii in range(2):
                                P.I("pool", "memset", ap=pT.ap[0:64, (2 * ii) * 128:(2 * ii + 1) * 128], constant=0.0, writes=[pT])
                        if lastb:
                            for ii in range(2):
                                P.I("pool", "memset", ap=pT.ap[64:128, (2 * ii + 1) * 128:(2 * ii + 2) * 128], constant=0.0, writes=[pT])
                        slot = jq % grp
                        if LV < 4:
                            continue
                        combos = [(ii, X) for ii in range(2) for X in range(2)]
                        for (dst_ps, lhs_of) in ((ps_o, lambda ii, X: Vp[ii].ap[:, c + X, :]),
                                                 (ps_d, lambda ii, X: self.onesH[ii].ap)):
                            for n_, (ii, X) in enumerate(combos):
                                P.I("pe", "matmul", out=dst_ps.ap[:, slot * 128:(slot + 1) * 128], lhsT=lhs_of(ii, X),
                                    rhs=pT.ap[:, (2 * ii + X) * 128:(2 * ii + X + 1) * 128], start=(n_ == 0), stop=(n_ == 3),
                                    reads=[pT, Vp[0], Vp[1]], writes=[dst_ps], inc=(n_ == 3))
                        if slot == grp - 1 and LV >= 5:
                            l0 = 128 * (jq - slot)
                            n = grp * 128
                            for (src_ps, acc) in ((ps_o, acc_n), (ps_d, acc_d)):
                                if d == 1:
                                    dst = acc.ap[:, l0:l0 + n]
                                else:
                                    dst = acc.ap.rearrange("p (l r) -> p r l", r=d)[:, r, l0:l0 + n]
                                if g == 0:
                                    P.I("act", "activation", out=dst, in_=src_ps.ap[:, 0:n], func=AF.Copy, reads=[src_ps], writes=[acc])
                                else:
                                    P.I("dve", "tensor_tensor", out=dst, in0=src_ps.ap[:, 0:n], in1=dst, op=ALU.add,
                                        reads=[src_ps, acc], writes=[acc])
            for t0 in range(0, S, TT):
                if LV < 6:
                    continue
                P.I("dve", "reciprocal", out=acc_d.ap[:, t0:t0 + TT], in_=acc_d.ap[:, t0:t0 + TT], reads=[acc_d], writes=[acc_d])
                P.I("dve", "tensor_tensor", out=aout.ap[:, t0:t0 + TT], in0=acc_n.ap[:, t0:t0 + TT], in1=acc_d.ap[:, t0:t0 + TT],
                    op=ALU.mult, reads=[acc_n, acc_d], writes=[aout])
            P.D("sp", self.ABC[hp, :, off:off + S], aout.ap, reads=[aout])
        P.barrier()
        ar.off = mark

    def mixer_b(self, l, si, ar, psb):
        P, io = self.P, self.io
        S, kind, idx = self.seqs[si]
        off = self.offs[si]
        mark = ar.off
        NB = S // 128
        sc = 128.0 ** -0.5
        WB = [ar.alloc([2048], BF16, f"wb{k}") for k in range(KD)]
        WBs = [ar.alloc([1024], BF16, f"wbs{k}") for k in range(KD)]
        for k in range(KD):
            self.load_w(WB[k], io["w_in"][l, k * 128:(k + 1) * 128, 2304:4352])
        for k in range(KD):
            for x in range(2):
                for h in range(4):
                    c0 = x * 512 + h * 128
                    P.I("act" if (h % 2 == 0) else "dve", "activation" if (h % 2 == 0) else "tensor_copy",
                        out=WBs[k].ap[:, c0:c0 + 64], in_=WB[k].ap[:, c0 + 64:c0 + 128],
                        reads=[WB[k]], writes=[WBs[k]], **({"func": AF.Copy} if h % 2 == 0 else {}))
                    P.I("dve" if (h % 2 == 0) else "act", "tensor_copy" if (h % 2 == 0) else "activation",
                        out=WBs[k].ap[:, c0 + 64:c0 + 128], in_=WB[k].ap[:, c0:c0 + 64],
                        reads=[WB[k]], writes=[WBs[k]], **({"func": AF.Copy} if h % 2 == 1 else {}))
        upool = ar.pool(2, [KD, TT], BF16, "uT")
        cosp = ar.pool(2, [TT], F32, "cos")
        sinp = ar.pool(2, [TT], F32, "sin")
        cq = ar.alloc([4096], F32, "cq")
        Dst = ar.alloc([2048], F32, "Dst")
        Dm = ar.alloc([2048], BF16, "Dm")
        dec = ar.alloc([8], F32, "dec")
        P.D("sp", cq.ap, io["c_retq"], writes=[cq])
        P.D("sp", Dst.ap, io["c_retD"], writes=[Dst])
        P.D("sp", dec.ap, io["c_retdec"], writes=[dec])
        P.I("dve", "tensor_copy", out=Dm.ap, in_=Dst.ap, reads=[Dst], writes=[Dm])
        P.I("dve", "tensor_scalar", out=dec.ap, in0=dec.ap, scalar1=sc, scalar2=None, op0=ALU.mult, reads=[dec], writes=[dec])
        qr, kr, qf, qb, gs = [ar.alloc([S], BF16, n) for n in ("qrT", "krT", "qfT", "qbT", "gsT")]
        vtok = ar.alloc([NB, 128], BF16, "vtok")
        Sall = [ar.alloc([NB + 1, 128], BF16, f"Sall{i}") for i in range(2)]
        Sst = [ar.alloc([128], F32, f"Sst{i}") for i in range(2)]
        bout = ar.alloc([S], BF16, "bout")
        t1p, t2p, ssp = ar.pool(2, [TT], F32, "t1"), ar.pool(2, [TT], F32, "t2"), ar.pool(2, [TT], F32, "ss")
        ktp = ar.pool(4, [128], BF16, "ktok")
        adp = ar.pool(2, [512], BF16, "AD")
        onp = ar.pool(4, [128], BF16, "on")
        stp = ar.pool(4, [6], F32, "bst")
        mvp = ar.pool(2, [8], F32, "mv")
        rsp = ar.pool(2, [4], F32, "rs")
        psp = Rot(psb[0:4])
        pskv = Rot(psb[4:6])
        pst = Rot(self.psbf)
        gain = self.gcol("ret_norm", l, 0, 4)
        for h in range(4):
            gfl = -math.expm1(math.log(2.0) * (-5.0 - h))
            gbl = -math.expm1(math.log(2.0) * (-5.5 - h))
            for t0 in range(0, S, TT):
                uT = upool.next()
                self.ld_fm(self.UT, off + t0, TT, uT)
                cs, sn = cosp.next(), sinp.next()
                P.D("sp", cs.ap, io["c_cos"][:, t0:t0 + TT], writes=[cs])
                P.D("sp", sn.ap, io["c_sin"][:, t0:t0 + TT], writes=[sn])
                for x in range(2):
                    p1, p2 = psp.next(), psp.next()
                    self.proj_tile(p1, WB, x * 512 + h * 128, uT)
                    self.proj_tile(p2, WBs, x * 512 + h * 128, uT)
                    t1, t2 = t1p.next(), t2p.next()
                    P.I("dve", "tensor_tensor", out=t1.ap, in0=p1.ap, in1=cs.ap, op=ALU.mult, reads=[p1, cs], writes=[t1])
                    P.I("dve", "tensor_tensor", out=t2.ap, in0=p2.ap, in1=sn.ap, op=ALU.mult, reads=[p2, sn], writes=[t2])
                    if x == 1:
                        P.I("dve", "tensor_tensor", out=kr.ap[:, t0:t0 + TT], in0=t1.ap, in1=t2.ap, op=ALU.add, reads=[t1, t2], writes=[kr])
                    else:
                        ss = ssp.next()
                        P.I("dve", "tensor_tensor", out=ss.ap, in0=t1.ap, in1=t2.ap, op=ALU.add, reads=[t1, t2], writes=[ss])
                        P.I("act", "activation", out=qr.ap[:, t0:t0 + TT], in_=ss.ap, func=AF.Copy, reads=[ss], writes=[qr])
                        P.I("dve", "tensor_tensor", out=qf.ap[:, t0:t0 + TT], in0=ss.ap, in1=cq.ap[:, (2 * h) * 512:(2 * h + 1) * 512],
                            op=ALU.mult, reads=[ss, cq], writes=[qf])
                        P.I("dve", "tensor_tensor", out=qb.ap[:, t0:t0 + TT], in0=ss.ap, in1=cq.ap[:, (2 * h + 1) * 512:(2 * h + 2) * 512],
                            op=ALU.mult, reads=[ss, cq], writes=[qb])
                pg = psp.next()
                self.proj_tile(pg, WB, 1536 + h * 128, uT)
                P.I("act", "activation", out=gs.ap[:, t0:t0 + TT], in_=pg.ap, func=AF.Silu, reads=[pg], writes=[gs])
                pv = psp.next()
                for sub in range(TT // 128):
                    self.mm_ap(pv.ap[:, sub * 128:(sub + 1) * 128], pv,
                               [(uT.ap[:, k, sub * 128:(sub + 1) * 128], WB[k].ap[:, 1024 + h * 128:1024 + (h + 1) * 128], [WB[k], uT])
                                for k in range(KD)])
                c0 = t0 // 128
                P.I("act", "activation", out=vtok.ap[:, c0:c0 + 4, :].rearrange("p a b -> p (a b)"), in_=pv.ap, func=AF.Copy,
                    reads=[pv], writes=[vtok])
            for dr in range(2):
                St = Sst[dr]
                SA = Sall[dr]
                gC = (gfl if dr == 0 else gbl) ** 128
                P.I("dve", "memset", ap=St.ap, constant=0.0, writes=[St])
                P.I("dve", "memset", ap=SA.ap[:, (0 if dr == 0 else NB), :], constant=0.0, writes=[SA])
                order = range(NB) if dr == 0 else range(NB - 1, -1, -1)
                for c in order:
                    pt = pst.next()
                    P.I("pe", "transpose", out=pt.ap[:, 0:128], in_=kr.ap[:, c * 128:(c + 1) * 128], identity=self.identb.ap,
                        reads=[kr, self.identb], writes=[pt])
                    kt = ktp.next()
                    P.I("act", "activation", out=kt.ap, in_=pt.ap[:, 0:128], func=AF.Copy, scale=dec.ap[:, 2 * h + dr:2 * h + dr + 1],
                        reads=[pt, dec], writes=[kt])
                    pk = pskv.next()
                    P.I("pe", "matmul", out=pk.ap[:, 0:128], lhsT=kt.ap, rhs=vtok.ap[:, c, :], start=True, stop=True,
                        reads=[kt, vtok], writes=[pk])
                    P.I("dve", "scalar_tensor_tensor", out=St.ap, in0=St.ap, scalar=float(gC), in1=pk.ap[:, 0:128],
                        op0=ALU.mult, op1=ALU.add, reads=[St, pk], writes=[St])
                    P.I("act", "activation", out=SA.ap[:, (c + 1 if dr == 0 else c), :], in_=St.ap, func=AF.Copy, reads=[St], writes=[SA])
            for c0 in range(0, NB, 4):
                pa, po = psp.next(), psp.next()
                for j in range(4):
                    c = c0 + j
                    P.I("pe", "matmul", out=pa.ap[:, j * 128:(j + 1) * 128], lhsT=kr.ap[:, c * 128:(c + 1) * 128],
                        rhs=qr.ap[:, c * 128:(c + 1) * 128], start=True, stop=True, reads=[kr, qr], writes=[pa])
                ad = adp.next()
                P.I("dve", "tensor_tensor", out=ad.ap, in0=pa.ap, in1=Dm.ap[:, h * 512:(h + 1) * 512], op=ALU.mult,
                    reads=[pa, Dm], writes=[ad])
                for j in range(4):
                    c = c0 + j
                    self.mm_ap(po.ap[:, j * 128:(j + 1) * 128], po,
                               [(ad.ap[:, j * 128:(j + 1) * 128], vtok.ap[:, c, :], [ad, vtok]),
                                (qf.ap[:, c * 128:(c + 1) * 128], Sall[0].ap[:, c, :], [qf, Sall[0]]),
                                (qb.ap[:, c * 128:(c + 1) * 128], Sall[1].ap[:, c + 1, :], [qb, Sall[1]])])
                mv, rs = mvp.next(), rsp.next()
                for j in range(4):
                    st6 = stp.next()
                    P.I("dve", "bn_stats", out=st6.ap, in_=po.ap[:, j * 128:(j + 1) * 128], reads=[po], writes=[st6])
                    P.I("dve", "bn_aggr", out=mv.ap[:, 2 * j:2 * j + 2], in_=st6.ap, reads=[st6], writes=[mv])
                P.I("act", "activation", out=rs.ap, in_=mv.ap.rearrange("p (j t) -> p j t", t=2)[:, :, 1], func=AF.Sqrt, bias=EPS, scale=1.0,
                    reads=[mv], writes=[rs])
                P.I("dve", "reciprocal", out=rs.ap, in_=rs.ap, reads=[rs], writes=[rs])
                pt = pst.next()
                for j in range(4):
                    on = onp.next()
                    P.I("dve", "tensor_scalar", out=on.ap, in0=po.ap[:, j * 128:(j + 1) * 128], scalar1=mv.ap[:, 2 * j:2 * j + 1],
                        scalar2=rs.ap[:, j:j + 1], op0=ALU.subtract, op1=ALU.mult, reads=[po, mv, rs], writes=[on])
                    P.I("pe", "transpose", out=pt.ap[:, j * 128:(j + 1) * 128], in_=on.ap, identity=self.identb.ap,
                        reads=[on, self.identb], writes=[pt])
                t0 = c0 * 128
                P.I("dve", "scalar_tensor_tensor", out=bout.ap[:, t0:t0 + 512], in0=pt.ap[:, 0:512], scalar=gain[:, h:h + 1],
                    in1=gs.ap[:, t0:t0 + 512], op0=ALU.mult, op1=ALU.mult, reads=[pt, gs], writes=[bout])
            P.D("sp", self.ABC[2 + h, :, off:off + S], bout.ap, reads=[bout])
        P.barrier()
        ar.off = mark

    def load_lb(self, ar):
        P, io = self.P, self.io
        dpf = self.depth_full
        ncol = self.depth * 4
        self.LB = ar.alloc([ncol], F32, "LB")
        self.OMLB = ar.alloc([ncol], F32, "OMLB")
        ex = ar.alloc([dpf * 4], F32, "lbex")
        sm = ar.alloc([4], F32, "lbsum")
        for l in range(dpf):
            P.I("act", "activation", out=ex.ap[:, l * 4:(l + 1) * 4], in_=self.gcol("hgrn_lower_bound", l, 0, 4), func=AF.Exp,
                reads=[self.G], writes=[ex])
        P.I("dve", "tensor_copy", out=sm.ap, in_=ex.ap[:, 0:4], reads=[ex], writes=[sm])
        for l in range(1, dpf):
            P.I("dve", "tensor_tensor", out=sm.ap, in0=sm.ap, in1=ex.ap[:, l * 4:(l + 1) * 4], op=ALU.add, reads=[sm, ex], writes=[sm])
        P.I("dve", "reciprocal", out=sm.ap, in_=sm.ap, reads=[sm], writes=[sm])
        for l in range(dpf):
            P.I("dve", "tensor_tensor", out=ex.ap[:, l * 4:(l + 1) * 4], in0=ex.ap[:, l * 4:(l + 1) * 4], in1=sm.ap, op=ALU.mult,
                reads=[ex, sm], writes=[ex])
        P.I("dve", "memset", ap=self.LB.ap, constant=0.0, writes=[self.LB])
        if self.mode == "L":
            lm = ar.alloc([4], F32, "lmask")
            P.D("sp", lm.ap, io["c_lmask"], writes=[lm])
            for l in range(dpf):
                P.I("dve", "scalar_tensor_tensor", out=self.LB.ap[:, 0:4], in0=ex.ap[:, l * 4:(l + 1) * 4], scalar=lm.ap[:, l:l + 1],
                    in1=self.LB.ap[:, 0:4], op0=ALU.mult, op1=ALU.add, reads=[ex, lm, self.LB], writes=[self.LB])
        else:
            for l in range(1, dpf):
                P.I("dve", "tensor_tensor", out=self.LB.ap[:, l * 4:(l + 1) * 4], in0=self.LB.ap[:, (l - 1) * 4:l * 4],
                    in1=ex.ap[:, l * 4:(l + 1) * 4], op=ALU.add, reads=[self.LB, ex], writes=[self.LB])
        P.I("dve", "tensor_scalar", out=self.OMLB.ap, in0=self.LB.ap, scalar1=-1.0, scalar2=1.0, op0=ALU.mult, op1=ALU.add,
            reads=[self.LB], writes=[self.OMLB])

    def mixer_c(self, l, si, ar, psb):
        P, io = self.P, self.io
        S, kind, idx = self.seqs[si]
        off = self.offs[si]
        mark = ar.off
        NC = S // 64
        WC = [ar.alloc([2560], BF16, f"wc{k}") for k in range(KD)]
        for k in range(KD):
            self.load_w(WC[k], io["w_in"][l, k * 128:(k + 1) * 128, 4352:6912])
        upool = ar.pool(1 if S > 2048 else 2, [KD, TT], BF16, "uT")
        reset = ar.alloc([512], F32, "reset")
        tri = ar.alloc([512], BF16, "tri")
        P.D("sp", reset.ap, io["c_reset"], writes=[reset])
        P.D("pool", tri.ap, io["c_tri"], writes=[tri])
        QT = [[ar.alloc([S], BF16, f"c{n}{dr}") for n in ("qt", "kt", "qh")] for dr in range(2)]
        gs = ar.alloc([S], BF16, "gsT")
        vtok = ar.alloc([NC, 128], BF16, "vtok")
        Sall = [ar.alloc([NC + 1, 128], BF16, f"Sall{i}") for i in range(2)]
        Sst = ar.alloc([128], F32, "Sst")
        cout = ar.alloc([S], BF16, "cout")
        qsp, kkp, lfp, Pp, ngp = [ar.pool(1, [TT], F32, n) for n in ("qs", "kk", "lf", "P", "ng")]
        etp = ar.alloc([TT], F32, "ET")
        e1p, e2p, e4p = [ar.pool(1, [TT], F32, n) for n in ("E1", "E2", "E4")]
        khp = ar.pool(2, [TT], BF16, "khat")
        ktokp = ar.pool(2, [1024], BF16, "khtok")
        attp = ar.pool(2, [512], BF16, "attm")
        sqj = ar.alloc([128], F32, "sqj")
        msp = ar.pool(2, [4], F32, "ms")
        onp = ar.pool(4, [128], BF16, "on")
        psp = Rot(psb[0:4])
        pskv = Rot(psb[4:6])
        pst = Rot(self.psbf)
        gain = self.gcol("hgrn_norm", l, 0, 4)
        for h in range(4):
            lbc = self.LB.ap[:, l * 4 + h:l * 4 + h + 1]
            omc = self.OMLB.ap[:, l * 4 + h:l * 4 + h + 1]
            for dr in range(2):
                St, SA = Sst, Sall[dr]
                P.I("dve", "memset", ap=St.ap, constant=0.0, writes=[St])
                P.I("dve", "memset", ap=SA.ap[:, (0 if dr == 0 else NC), :], constant=0.0, writes=[SA])
                tiles = range(0, S, TT) if dr == 0 else range(S - TT, -1, -TT)
                for t0 in tiles:
                    uT = upool.next()
                    self.ld_fm(self.UT, off + t0, TT, uT)
                    pq, pz = psp.next(), psp.next()
                    self.proj_tile(pq, WC, 0 + h * 128, uT)
                    self.proj_tile(pz, WC, (512 if dr == 0 else 1024) + h * 128, uT)
                    qs, kk, lf, Pc, ng = qsp.next(), kkp.next(), lfp.next(), Pp.next(), ngp.next()
                    P.I("act", "activation", out=qs.ap, in_=pq.ap, func=AF.Silu, reads=[pq], writes=[qs])
                    P.I("act", "activation", out=kk.ap, in_=pz.ap, func=AF.Sigmoid, reads=[pz], writes=[kk])
                    P.I("dve", "tensor_scalar", out=kk.ap, in0=kk.ap, scalar1=omc, scalar2=lbc, op0=ALU.mult, op1=ALU.add,
                        reads=[kk, self.LB, self.OMLB], writes=[kk])
                    P.I("act", "activation", out=lf.ap, in_=kk.ap, func=AF.Ln, reads=[kk], writes=[lf])
                    P.I("dve", "tensor_scalar", out=kk.ap, in0=kk.ap, scalar1=-1.0, scalar2=1.0, op0=ALU.mult, op1=ALU.add,
                        reads=[kk], writes=[kk])
                    P.I("dve", "tensor_tensor_scan", out=Pc.ap, data0=reset.ap, data1=lf.ap, initial=0.0, op0=ALU.mult, op1=ALU.add,
                        reads=[reset, lf], writes=[Pc])
                    e1, e2, e4 = e1p.next(), e2p.next(), e4p.next()
                    if dr == 0:
                        P.I("dve", "tensor_scalar", out=ng.ap, in0=Pc.ap, scalar1=-1.0, scalar2=None, op0=ALU.mult, reads=[Pc], writes=[ng])
                        P.I("act", "activation", out=etp.ap, in_=Pc.ap, func=AF.Exp, reads=[Pc], writes=[etp])
                        for j in range(TT // 64):
                            cs = slice(j * 64, (j + 1) * 64)
                            P.I("act", "activation", out=e1.ap[:, cs], in_=Pc.ap[:, cs], func=AF.Exp, bias=ng.ap[:, j * 64 + 31:j * 64 + 32],
                                scale=1.0, reads=[Pc, ng], writes=[e1])
                            P.I("act", "activation", out=e2.ap[:, cs], in_=Pc.ap[:, cs], func=AF.Exp, bias=Pc.ap[:, j * 64 + 31:j * 64 + 32],
                                scale=-1.0, reads=[Pc], writes=[e2])
                            P.I("act", "activation", out=e4.ap[:, cs], in_=Pc.ap[:, cs], func=AF.Exp, bias=Pc.ap[:, j * 64 + 63:j * 64 + 64],
                                scale=-1.0, reads=[Pc], writes=[e4])
                    else:
                        P.I("dve", "tensor_tensor", out=lf.ap, in0=Pc.ap, in1=lf.ap, op=ALU.subtract, reads=[Pc, lf], writes=[lf])
                        P.I("dve", "tensor_scalar", out=ng.ap, in0=lf.ap, scalar1=-1.0, scalar2=None, op0=ALU.mult, reads=[lf], writes=[ng])
                        P.I("act", "activation", out=etp.ap, in_=Pc.ap, func=AF.Exp, reads=[Pc], writes=[etp])
                        P.I("act", "activation", out=e4.ap, in_=lf.ap, func=AF.Exp, reads=[lf], writes=[e4])
                        for j in range(TT // 64):
                            cs = slice(j * 64, (j + 1) * 64)
                            P.I("act", "activation", out=e1.ap[:, cs], in_=lf.ap[:, cs], func=AF.Exp, bias=lf.ap[:, j * 64 + 32:j * 64 + 33],
                                scale=-1.0, reads=[lf], writes=[e1])
                            P.I("act", "activation", out=e2.ap[:, cs], in_=lf.ap[:, cs], func=AF.Exp, bias=ng.ap[:, j * 64 + 32:j * 64 + 33],
                                scale=1.0, reads=[lf, ng], writes=[e2])
                    qt, kt, qh = QT[dr]
                    ts = slice(t0, t0 + TT)
                    P.I("dve", "tensor_tensor", out=qt.ap[:, ts], in0=qs.ap, in1=e1.ap, op=ALU.mult, reads=[qs, e1], writes=[qt])
                    P.I("dve", "tensor_tensor", out=kt.ap[:, ts], in0=kk.ap, in1=e2.ap, op=ALU.mult, reads=[kk, e2], writes=[kt])
                    if dr == 0:
                        P.I("dve", "tensor_tensor", out=qh.ap[:, ts], in0=qs.ap, in1=etp.ap, op=ALU.mult, reads=[qs, etp], writes=[qh])
                    else:
                        for j in range(TT // 64):
                            cs = slice(j * 64, (j + 1) * 64)
                            P.I("act", "activation", out=e1.ap[:, cs], in_=lf.ap[:, cs], func=AF.Exp, bias=Pc.ap[:, j * 64 + 63:j * 64 + 64],
                                scale=-1.0, reads=[lf, Pc, qt], writes=[e1])
                        P.I("dve", "tensor_tensor", out=qh.ap[:, ts], in0=qs.ap, in1=e1.ap, op=ALU.mult, reads=[qs, e1], writes=[qh])
                    kh = khp.next()
                    P.I("dve", "tensor_tensor", out=kh.ap, in0=kk.ap, in1=e4.ap, op=ALU.mult, reads=[kk, e4], writes=[kh])
                    if dr == 0:
                        pg = psp.next()
                        self.proj_tile(pg, WC, 2048 + h * 128, uT)
                        P.I("act", "activation", out=gs.ap[:, ts], in_=pg.ap, func=AF.Silu, reads=[pg], writes=[gs])
                        for half in range(2):
                            pv = psp.next()
                            for jj in range(4):
                                j = half * 4 + jj
                                self.mm_ap(pv.ap[0:64, jj * 128:(jj + 1) * 128], pv,
                                           [(uT.ap[:, k, j * 64:(j + 1) * 64], WC[k].ap[:, 1536 + h * 128:1536 + (h + 1) * 128], [WC[k], uT])
                                            for k in range(KD)])
                            cc = t0 // 64 + half * 4
                            P.I("act", "activation", out=vtok.ap[0:64, cc:cc + 4, :].rearrange("p a b -> p (a b)"), in_=pv.ap[0:64, :],
                                func=AF.Copy, reads=[pv], writes=[vtok])
                    pt = pst.next()
                    for j in range(TT // 64):
                        P.I("pe", "transpose", out=pt.ap[0:64, j * 128:(j + 1) * 128], in_=kh.ap[:, j * 64:(j + 1) * 64],
                            identity=self.identb.ap, reads=[kh, self.identb], writes=[pt])
                    ktok = ktokp.next()
                    P.I("act", "activation", out=ktok.ap[0:64, :], in_=pt.ap[0:64, :], func=AF.Copy, reads=[pt], writes=[ktok])
                    for half in (range(2) if dr == 0 else range(1, -1, -1)):
                        pk = pskv.next()
                        for jj in range(4):
                            j = half * 4 + jj
                            c = t0 // 64 + j
                            P.I("pe", "matmul", out=pk.ap[:, jj * 128:(jj + 1) * 128], lhsT=ktok.ap[0:64, j * 128:(j + 1) * 128],
                                rhs=vtok.ap[0:64, c, :], start=True, stop=True, reads=[ktok, vtok], writes=[pk])
                        for jj in (range(4) if dr == 0 else range(3, -1, -1)):
                            j = half * 4 + jj
                            c = t0 // 64 + j
                            P.I("dve", "scalar_tensor_tensor", out=St.ap, in0=St.ap, scalar=etp.ap[:, j * 64 + 63:j * 64 + 64],
                                in1=pk.ap[:, jj * 128:(jj + 1) * 128], op0=ALU.mult, op1=ALU.add, reads=[St, etp, pk], writes=[St])
                            P.I("act", "activation", out=SA.ap[:, (c + 1 if dr == 0 else c), :], in_=St.ap, func=AF.Copy,
                                reads=[St], writes=[SA])
            for c0 in range(0, NC, 4):
                pa, po = psp.next(), psp.next()
                for jj in range(4):
                    c = c0 + jj
                    cs = slice(c * 64, (c + 1) * 64)
                    for dr in range(2):
                        P.I("pe", "matmul", out=pa.ap[0:64, (2 * jj + dr) * 64:(2 * jj + dr + 1) * 64], lhsT=QT[dr][1].ap[:, cs],
                            rhs=QT[dr][0].ap[:, cs], start=True, stop=True, reads=[QT[dr][0], QT[dr][1]], writes=[pa])
                att = attp.next()
                P.I("dve", "tensor_tensor", out=att.ap[0:64, :], in0=pa.ap[0:64, :], in1=tri.ap[0:64, :], op=ALU.mult,
                    reads=[pa, tri], writes=[att])
                for jj in range(4):
                    c = c0 + jj
                    cs = slice(c * 64, (c + 1) * 64)
                    self.mm_ap(po.ap[0:64, jj * 128:(jj + 1) * 128], po,
                               [(att.ap[0:64, (2 * jj) * 64:(2 * jj + 1) * 64], vtok.ap[0:64, c, :], [att, vtok]),
                                (att.ap[0:64, (2 * jj + 1) * 64:(2 * jj + 2) * 64], vtok.ap[0:64, c, :], [att, vtok]),
                                (QT[0][2].ap[:, cs], Sall[0].ap[:, c, :], [QT[0][2], Sall[0]]),
                                (QT[1][2].ap[:, cs], Sall[1].ap[:, c + 1, :], [QT[1][2], Sall[1]])])
                ms = msp.next()
                P.I("dve", "memset", ap=ms.ap, constant=0.0, writes=[ms])
                for jj in range(4):
                    P.I("act", "activation", out=sqj.ap[0:64, :], in_=po.ap[0:64, jj * 128:(jj + 1) * 128], func=AF.Square,
                        accum_out=ms.ap[0:64, jj:jj + 1], reads=[po], writes=[sqj, ms])
                P.I("act", "activation", out=ms.ap[0:64, :], in_=ms.ap[0:64, :], func=AF.Sqrt, bias=EPS, scale=1.0 / 128, reads=[ms], writes=[ms])
                P.I("dve", "reciprocal", out=ms.ap[0:64, :], in_=ms.ap[0:64, :], reads=[ms], writes=[ms])
                pt = pst.next()
                for jj in range(4):
                    on = onp.next()
                    P.I("dve", "tensor_scalar", out=on.ap[0:64, :], in0=po.ap[0:64, jj * 128:(jj + 1) * 128], scalar1=ms.ap[0:64, jj:jj + 1],
                        scalar2=None, op0=ALU.mult, reads=[po, ms], writes=[on])
                    P.I("pe", "transpose", out=pt.ap[:, jj * 64:(jj + 1) * 64], in_=on.ap[0:64, :], identity=self.identb.ap[0:64, 0:64],
                        reads=[on, self.identb], writes=[pt])
                t0 = c0 * 64
                P.I("dve", "scalar_tensor_tensor", out=cout.ap[:, t0:t0 + 256], in0=pt.ap[:, 0:256], scalar=gain[:, h:h + 1],
                    in1=gs.ap[:, t0:t0 + 256], op0=ALU.mult, op1=ALU.mult, reads=[pt, gs], writes=[cout])
            P.D("sp", self.ABC[6 + h, :, off:off + S], cout.ap, reads=[cout])
        P.barrier()
        ar.off = mark

    def build(self, phases=None):
        nc, P = self.nc, self.P
        self.offs = self.seq_offsets()
        self.declare()
        with contextlib.ExitStack() as st:
            arena_t = st.enter_context(nc.sbuf_tensor("arena", [128, 53000], F32))
            psum_t = st.enter_context(nc.psum_tensor("psum", [128, 6, 512], F32))
            psum_b = st.enter_context(nc.psum_tensor("psumb", [128, 2, 1024], BF16))
            ar = Arena(arena_t[:, :])
            psb = [Buf(psum_t[:, i, :], name=f"ps{i}") for i in range(6)]
            self.psbf = [Buf(psum_b[:, i, :], name=f"psbf{i}") for i in range(2)]
            self.psum_t = psum_t
            self.load_consts(ar, psb)
            P.barrier()
            mode = self.mode
            if mode == "X":
                self.phase_x(ar, psb)
                self.copy_ht(self.HT, self.io["HT_out"])
            elif mode == "F":
                self.copy_ht(self.io["HT_in"], self.HT)
                self.phase_final(ar, psb)
            else:
                self.load_att_consts(ar, psb)
                self.load_lb(ar)
                P.barrier()
                if mode == "L":
                    self.copy_ht(self.io["HT_in"], self.HT)
                elif not (phases and "noX" in phases):
                    self.phase_x(ar, psb)
                for l in range(self.depth):
                    last = (l == self.depth - 1) and mode == "ALL"
                    for ph in (phases or ("A", "B", "C", "D", "E")):
                        if ph == "A":
                            self.phase_ffn(l, 1, ar, psb)
                        elif ph == "B":
                            self.phase_mix(l, ar, psb)
                        elif ph == "C":
                            self.phase_merge(l, ar, psb)
                        elif ph == "D":
                            self.phase_ffn(l, 2, ar, psb)
                        elif ph == "E":
                            self.phase_ple(l, ar, psb, last)
                if mode == "L":
                    self.copy_ht(self.HT, self.io["HT_out"])
            P.barrier()
            P.emit()
        return nc


N_CORES = 8
_WEIGHT_NAMES = ("ffn1_norm", "ffn1_w_gate", "ffn1_w_up", "ffn1_w_down", "mix_norm", "w_in", "rel_bias", "ret_norm",
                 "hgrn_lower_bound", "hgrn_norm", "w_branch_a", "w_branch_b", "w_branch_c", "w_merge_gate", "w_out",
                 "ffn2_norm", "ffn2_w_gate", "ffn2_w_up", "ffn2_w_down", "ple_norm", "w_ple_gate", "w_ple_proj", "final_norm")


_LAYER_W = ("ffn1_norm", "ffn1_w_gate", "ffn1_w_up", "ffn1_w_down", "mix_norm", "w_in", "ret_norm", "hgrn_norm",
            "w_branch_a", "w_branch_b", "w_branch_c", "w_merge_gate", "w_out", "ffn2_norm", "ffn2_w_gate", "ffn2_w_up",
            "ffn2_w_down", "ple_norm", "w_ple_gate", "w_ple_proj")


def run_step(inputs, n_cores=N_CORES):
    xp, xs = inputs["x_prompt"], inputs["x_sample"]
    pp, ps_ = inputs["p_prompt"], inputs["p_sample"]
    depth = pp.shape[0]
    nb_p, nb_s = xp.shape[0] // n_cores, xs.shape[0] // n_cores
    seqs = [(xp.shape[1], "p", i) for i in range(nb_p)] + [(xs.shape[1], "s", i) for i in range(nb_s)]
    consts = host_consts()
    cores = list(range(n_cores))
    f32 = lambda a: np.ascontiguousarray(a, dtype=np.float32)
    bx = Builder(seqs, 1, mode="X")
    ncx = bx.build()
    maps = []
    for c in cores:
        m = {"c_ident": consts["c_ident"]}
        if nb_p:
            m["x_prompt"] = f32(xp[c * nb_p:(c + 1) * nb_p])
        if nb_s:
            m["x_sample"] = f32(xs[c * nb_s:(c + 1) * nb_s])
        maps.append(m)
    res = run_bass_kernel_spmd(ncx, maps, core_ids=cores)
    ht = [np.asarray(r["HT_out"]) for r in res.results]
    bl = Builder(seqs, 1, mode="L", depth_full=depth)
    ncl = bl.build()
    for l in range(depth):
        shared = {k: f32(inputs[k][l:l + 1]) for k in _LAYER_W}
        shared["rel_bias"] = f32(inputs["rel_bias"])
        shared["hgrn_lower_bound"] = f32(inputs["hgrn_lower_bound"])
        lm = np.zeros((128, 4), np.float32)
        lm[:, 1:l + 1] = 1.0
        shared["c_lmask"] = lm
        for k, v in consts.items():
            shared[k] = v
        maps = []
        for c in cores:
            m = dict(shared)
            m["HT_in"] = ht[c]
            if nb_p:
                m["p_prompt"] = f32(pp[l:l + 1, c * nb_p:(c + 1) * nb_p])
            if nb_s:
                m["p_sample"] = f32(ps_[l:l + 1, c * nb_s:(c + 1) * nb_s])
            maps.append(m)
        res = run_bass_kernel_spmd(ncl, maps, core_ids=cores)
        ht = [np.asarray(r["HT_out"]) for r in res.results]
    bf = Builder(seqs, 1, mode="F")
    ncf = bf.build()
    maps = [{"c_ident": consts["c_ident"], "final_norm": f32(inputs["final_norm"]), "HT_in": ht[c]} for c in cores]
    res = run_bass_kernel_spmd(ncf, maps, core_ids=cores)
    outs = []
    for nm, nb in (("y_prompt", nb_p), ("y_sample", nb_s)):
        if nb:
            outs.append(np.concatenate([np.asarray(r[nm]) for r in res.results], axis=0).astype(np.float32))
    return tuple(outs)


def kernel(**inputs):
    return run_step(inputs)
```
